# Optimizing a Trainium2 kernel written in Bass

```python
import jax
import jax.numpy as jnp
from jax import lax
import numpy as np

D_MODEL = 1024
BATCH = 4
SEQ = 4096
DEPTH = 4

GRID_W = 64
CTX_LEN = 256
HEAD_DIM = 64
EPS = 1e-6
MASK_VALUE = -1e30
ROPE_THETA = 10000.0
N_MOD = 9
D_FF = 256 * ((8 * D_MODEL // 3 + 255) // 256)
HG_WIDTH = D_MODEL // 4
HG_DK = 64
HG_HEADS = HG_WIDTH // HG_DK
HG_DV = HG_WIDTH // HG_HEADS
HG_CHUNK = 64
NA_WIDTH = 3 * D_MODEL // 8
NA_HEADS = NA_WIDTH // HEAD_DIM
NA_ROWS = 8
NA_COLS = 16
SW_WIDTH = D_MODEL - HG_WIDTH - NA_WIDTH
SW_HEADS = SW_WIDTH // HEAD_DIM
SW_KV_HEADS = 2
SW_KV_WIDTH = SW_KV_HEADS * HEAD_DIM
SW_WINDOW = 128
SW_BLOCK = 128
MIX_WIDTH = HG_WIDTH + NA_WIDTH + SW_WIDTH
IN_SPLITS = (HG_WIDTH,) * 5 + (NA_WIDTH,) * 3 + (SW_WIDTH, SW_KV_WIDTH, SW_KV_WIDTH)
IN_WIDTH = sum(IN_SPLITS)

kernel_name = 'hybrid_hgrn2_natten_swa_prefix_dit'


def _rms_norm(x, g):
    xf = x.astype(jnp.float32)
    y = xf * lax.rsqrt(jnp.mean(xf * xf, axis=-1, keepdims=True) + EPS)
    return y.astype(x.dtype) * g


def _modulate(x, g, shift, scale):
    return _rms_norm(x, g) * (1 + scale) + shift


def _swiglu(h, w1, w3, w2):
    return (jax.nn.silu(h @ w1) * (h @ w3)) @ w2


def _softmax32(s):
    return jax.nn.softmax(s.astype(jnp.float32), axis=-1)


def _heads(a, n):
    b, t, _ = a.shape
    return a.reshape(b, t, n, -1).transpose(0, 2, 1, 3)


def _merge(a):
    b, h, t, d = a.shape
    return a.transpose(0, 2, 1, 3).reshape(b, t, h * d)


def _split_cols(p):
    return jnp.split(p, [int(s) for s in np.cumsum(IN_SPLITS)[:-1]], axis=-1)


def _axial_rope(x):
    t = x.shape[2]
    pos = jnp.arange(t)
    pos = jnp.stack([pos // GRID_W, pos % GRID_W], axis=-1).astype(jnp.float32)
    nf = HEAD_DIM // 4
    inv = ROPE_THETA ** (-jnp.arange(nf, dtype=jnp.float32) / nf)
    ang = pos[:, :, None] * inv
    cos, sin = jnp.cos(ang), jnp.sin(ang)
    xs = x.astype(jnp.float32).reshape(*x.shape[:-1], 2, 2, nf)
    x1, x2 = xs[..., 0, :], xs[..., 1, :]
    out = jnp.stack([x1 * cos - x2 * sin, x1 * sin + x2 * cos], axis=-2)
    return out.reshape(x.shape).astype(x.dtype)


def _context_attention(q, k, v, sink):
    s = jnp.einsum('bhgqd,bhkd->bhgqk', q, k).astype(jnp.float32) * q.shape[-1] ** -0.5
    n = s.shape[-1]
    if sink is not None:
        s = jnp.concatenate([s, jnp.broadcast_to(sink[None, :, :, None, None], s.shape[:-1] + (1,))], axis=-1)
    p = _softmax32(s)[..., :n].astype(v.dtype)
    return jnp.einsum('bhgqk,bhkd->bhgqd', p, v)


def _hgrn_gates(z, lb):
    z = z.astype(jnp.float32)
    log_f = jnp.log(lb + (1.0 - lb) * jax.nn.sigmoid(z))
    k = (1.0 - lb) * jax.nn.sigmoid(-z)
    return log_f, k


def _gla_chunk_scan(q, k, v, log_f, s0, with_output):
    b, h, t, _ = q.shape
    n = t // HG_CHUNK

    def blocks(a):
        return jnp.moveaxis(a.reshape(b, h, n, HG_CHUNK, a.shape[-1]), 2, 0)

    lower = jnp.tril(jnp.ones((HG_CHUNK, HG_CHUNK), bool))

    def step(s, inp):
        qc, kc, vc, gc = inp
        cum = jnp.cumsum(gc, axis=2)
        last = cum[:, :, -1]
        s_new = jnp.exp(last)[..., None] * s + jnp.einsum('bhsk,bhsv->bhkv', kc * jnp.exp(last[:, :, None] - cum), vc)
        if not with_output:
            return s_new, None
        o_inter = jnp.einsum('bhtk,bhkv->bhtv', qc * jnp.exp(cum), s)
        rel = jnp.where(lower[:, :, None], cum[:, :, :, None, :] - cum[:, :, None, :, :], MASK_VALUE)
        a = jnp.einsum('bhtk,bhtsk,bhsk->bhts', qc, jnp.exp(rel), kc)
        return s_new, o_inter + jnp.einsum('bhts,bhsv->bhtv', a, vc)

    s, o = lax.scan(step, s0, (blocks(q), blocks(k), blocks(v), blocks(log_f)))
    if with_output:
        o = jnp.moveaxis(o, 0, 2).reshape(b, h, t, -1)
    return s, o


def _hgrn_readout(o, g, norm_g):
    o = o * lax.rsqrt(jnp.mean(o * o, axis=-1, keepdims=True) + EPS)
    o = _merge(o) * norm_g.astype(jnp.float32)
    return (o * jax.nn.silu(g.astype(jnp.float32))).astype(g.dtype)


def _hgrn2_mixer(lat, ctx, lb, norm_g, need_ctx):
    def prep(parts):
        q, z_f, z_b, i, g = parts
        lf_f, k_f = _hgrn_gates(z_f, lb[0])
        lf_b, k_b = _hgrn_gates(z_b, lb[1])
        hd = lambda a: _heads(a, HG_HEADS)
        return (hd(jax.nn.silu(q.astype(jnp.float32))), hd(i.astype(jnp.float32)),
                hd(lf_f), hd(k_f), hd(lf_b), hd(k_b), g)

    def flip(a):
        return a[:, :, ::-1]

    def bidir(parts, s_f, s_b, with_output):
        q, v, lf_f, k_f, lf_b, k_b, g = prep(parts)
        s_f, o_f = _gla_chunk_scan(q, k_f, v, lf_f, s_f, with_output)
        s_b, o_b = _gla_chunk_scan(flip(q), flip(k_b), flip(v), flip(lf_b), s_b, with_output)
        out = _hgrn_readout(o_f + flip(o_b), g, norm_g) if with_output else None
        return s_f, s_b, out

    s0 = jnp.zeros((lat[0].shape[0], HG_HEADS, HG_DK, HG_DV), jnp.float32)
    s_f, s_b, out_ctx = bidir(ctx, s0, s0, need_ctx)
    _, _, out_lat = bidir(lat, s_f, s_b, True)
    return out_lat, out_ctx


def _neighbourhood_attention(q, k, v, qc, kc, vc, rpb, need_ctx):
    b, h, t, dh = q.shape
    rows = t // GRID_W
    wr = min(NA_ROWS, rows)
    n_keys = wr * GRID_W
    scale = dh ** -0.5
    r = jnp.arange(rows)
    row_idx = jnp.clip(r - wr // 2, 0, rows - wr)[:, None] + jnp.arange(wr)[None, :]

    def gather_rows(a):
        return a.reshape(b, h, rows, GRID_W, dh)[:, :, row_idx].reshape(b, h, rows, n_keys, dh)

    kg, vg = gather_rows(k), gather_rows(v)
    qg = q.reshape(b, h, rows, GRID_W, dh)
    col = jnp.arange(GRID_W)
    c0 = jnp.clip(col - NA_COLS // 2, 0, GRID_W - NA_COLS)
    col_ok = (col[None, :] >= c0[:, None]) & (col[None, :] < c0[:, None] + NA_COLS)
    d_row = row_idx - r[:, None]
    d_col = jnp.clip(col[None, :] - col[:, None], 1 - NA_COLS, NA_COLS - 1)
    bias = rpb[:, d_row[:, None, :, None] + NA_ROWS - 1, d_col[None, :, None, :] + NA_COLS - 1].astype(jnp.float32)
    bias = jnp.where(col_ok[None, None, :, None, :], bias, MASK_VALUE).reshape(h, rows, GRID_W, n_keys)
    s_lat = jnp.einsum('bhrqd,bhrkd->bhrqk', qg, kg).astype(jnp.float32) * scale + bias
    s_ctx = jnp.einsum('bhrqd,bhld->bhrql', qg, kc).astype(jnp.float32) * scale
    p = _softmax32(jnp.concatenate([s_lat, s_ctx], axis=-1)).astype(v.dtype)
    o = (jnp.einsum('bhrqk,bhrkd->bhrqd', p[..., :n_keys], vg)
         + jnp.einsum('bhrql,bhld->bhrqd', p[..., n_keys:], vc))
    o = o.reshape(b, h, t, dh)
    oc = _context_attention(qc[:, :, None], kc, vc, None)[:, :, 0] if need_ctx else None
    return o, oc


def _sliding_window_attention(q, k, v, qc, kc, vc, sink, need_ctx):
    b, hq, t, dh = q.shape
    hkv = k.shape[1]
    g = hq // hkv
    n = t // SW_BLOCK
    nk = 3 * SW_BLOCK
    scale = dh ** -0.5
    sink_g = sink.reshape(hkv, g).astype(jnp.float32)
    qb = q.reshape(b, hkv, g, n, SW_BLOCK, dh)

    def band(a):
        ap = jnp.pad(a, ((0, 0), (0, 0), (SW_BLOCK, SW_BLOCK), (0, 0))).reshape(b, hkv, n + 2, SW_BLOCK, dh)
        return jnp.concatenate([ap[:, :, :-2], ap[:, :, 1:-1], ap[:, :, 2:]], axis=3)

    kb, vb = band(k), band(v)
    blk = jnp.arange(n)[:, None]
    q_pos = blk * SW_BLOCK + jnp.arange(SW_BLOCK)[None, :]
    k_pos = (blk - 1) * SW_BLOCK + jnp.arange(nk)[None, :]
    ok = ((jnp.abs(q_pos[:, :, None] - k_pos[:, None, :]) <= SW_WINDOW)
          & (k_pos[:, None, :] >= 0) & (k_pos[:, None, :] < t))
    s_lat = jnp.where(ok, jnp.einsum('bhgnqd,bhnkd->bhgnqk', qb, kb).astype(jnp.float32) * scale, MASK_VALUE)
    s_ctx = jnp.einsum('bhgnqd,bhld->bhgnql', qb, kc).astype(jnp.float32) * scale
    s_sink = jnp.broadcast_to(sink_g[None, :, :, None, None, None], s_ctx.shape[:-1] + (1,))
    p = _softmax32(jnp.concatenate([s_lat, s_ctx, s_sink], axis=-1)).astype(v.dtype)
    n_ctx = kc.shape[2]
    o = (jnp.einsum('bhgnqk,bhnkd->bhgnqd', p[..., :nk], vb)
         + jnp.einsum('bhgnql,bhld->bhgnqd', p[..., nk:nk + n_ctx], vc))
    o = o.reshape(b, hq, t, dh)
    oc = _context_attention(qc.reshape(b, hkv, g, -1, dh), kc, vc, sink_g).reshape(b, hq, -1, dh) if need_ctx else None
    return o, oc


def _mixer_block(hx, hc, w_in, lb, hg_norm_g, rpb, sink, need_ctx):
    px = _split_cols(hx @ w_in)
    pc = _split_cols(hc @ w_in)
    o_hg, oc_hg = _hgrn2_mixer(px[0:5], pc[0:5], lb, hg_norm_g, need_ctx)
    na_lat = [_heads(a, NA_HEADS) for a in px[5:8]]
    na_ctx = [_heads(a, NA_HEADS) for a in pc[5:8]]
    o_na, oc_na = _neighbourhood_attention(*na_lat, *na_ctx, rpb, need_ctx)
    sw_q = _axial_rope(_heads(px[8], SW_HEADS))
    sw_k = _axial_rope(_heads(px[9], SW_KV_HEADS))
    sw_v = _heads(px[10], SW_KV_HEADS)
    o_sw, oc_sw = _sliding_window_attention(sw_q, sw_k, sw_v, _heads(pc[8], SW_HEADS), _heads(pc[9], SW_KV_HEADS),
                                            _heads(pc[10], SW_KV_HEADS), sink, need_ctx)
    out_lat = jnp.concatenate([o_hg, _merge(o_na), _merge(o_sw)], axis=-1)
    out_ctx = jnp.concatenate([oc_hg, _merge(oc_na), _merge(oc_sw)], axis=-1) if need_ctx else None
    return out_lat, out_ctx


def setup_inputs(seed: int = 0) -> dict:
    key = jax.random.key(seed)
    ks = jax.random.split(key, 17)
    f32 = jnp.float32

    def nrm(k, shape, scale):
        return jax.random.normal(k, shape, f32) * scale

    return {
        'x': nrm(ks[0], (BATCH, SEQ, D_MODEL), 1.0),
        'c': nrm(ks[1], (BATCH, D_MODEL), 1.0),
        'ctx': nrm(ks[2], (BATCH, CTX_LEN, D_MODEL), 1.0),
        'c_ctx': nrm(ks[3], (D_MODEL,), 1.0),
        'ada_w': nrm(ks[4], (DEPTH, D_MODEL, N_MOD * D_MODEL), 0.5 * D_MODEL ** -0.5),
        'ada_b': nrm(ks[5], (DEPTH, N_MOD * D_MODEL), 0.02),
        'norm_g': 1.0 + nrm(ks[6], (DEPTH, 3, D_MODEL), 0.02),
        'ffn_w1': nrm(ks[7], (DEPTH, 2, D_MODEL, D_FF), D_MODEL ** -0.5),
        'ffn_w3': nrm(ks[8], (DEPTH, 2, D_MODEL, D_FF), D_MODEL ** -0.5),
        'ffn_w2': nrm(ks[9], (DEPTH, 2, D_FF, D_MODEL), D_FF ** -0.5),
        'w_in': nrm(ks[10], (DEPTH, D_MODEL, IN_WIDTH), D_MODEL ** -0.5),
        'w_out': nrm(ks[11], (DEPTH, MIX_WIDTH, D_MODEL), MIX_WIDTH ** -0.5),
        'hg_lb_logits': nrm(ks[12], (DEPTH, 2, HG_WIDTH), 0.5),
        'hg_norm_g': 1.0 + nrm(ks[13], (DEPTH, HG_WIDTH), 0.02),
        'na_rpb': nrm(ks[14], (DEPTH, NA_HEADS, 2 * NA_ROWS - 1, 2 * NA_COLS - 1), 0.1),
        'sw_sink': nrm(ks[15], (DEPTH, SW_HEADS), 0.5),
        'final_g': 1.0 + nrm(ks[16], (D_MODEL,), 0.02),
    }


def reference(x, c, ctx, c_ctx, ada_w, ada_b, norm_g, ffn_w1, ffn_w3, ffn_w2, w_in, w_out,
              hg_lb_logits, hg_norm_g, na_rpb, sw_sink, final_g):
    lb_soft = jax.nn.softmax(hg_lb_logits.astype(jnp.float32), axis=0)
    lower_bounds = jnp.cumsum(lb_soft, axis=0) - lb_soft[0]
    s_c = jax.nn.silu(c)
    s_cc = jax.nn.silu(c_ctx)
    h = ctx
    for l in range(DEPTH):
        need_ctx = l < DEPTH - 1
        m = jnp.split((s_c @ ada_w[l] + ada_b[l])[:, None, :], N_MOD, axis=-1)
        mc = jnp.split(s_cc @ ada_w[l] + ada_b[l], N_MOD, axis=-1)
        x = x + 0.5 * m[2] * _swiglu(_modulate(x, norm_g[l, 0], m[0], m[1]), ffn_w1[l, 0], ffn_w3[l, 0], ffn_w2[l, 0])
        h = h + 0.5 * mc[2] * _swiglu(_modulate(h, norm_g[l, 0], mc[0], mc[1]), ffn_w1[l, 0], ffn_w3[l, 0], ffn_w2[l, 0])
        o_lat, o_ctx = _mixer_block(_modulate(x, norm_g[l, 1], m[3], m[4]), _modulate(h, norm_g[l, 1], mc[3], mc[4]),
                                    w_in[l], lower_bounds[l], hg_norm_g[l], na_rpb[l], sw_sink[l], need_ctx)
        x = x + m[5] * (o_lat @ w_out[l])
        x = x + 0.5 * m[8] * _swiglu(_modulate(x, norm_g[l, 2], m[6], m[7]), ffn_w1[l, 1], ffn_w3[l, 1], ffn_w2[l, 1])
        if need_ctx:
            h = h + mc[5] * (o_ctx @ w_out[l])
            h = h + 0.5 * mc[8] * _swiglu(_modulate(h, norm_g[l, 2], mc[6], mc[7]), ffn_w1[l, 1], ffn_w3[l, 1], ffn_w2[l, 1])
    return _rms_norm(x, final_g)
```

```python
import bisect
import os
import contextlib
import numpy as np
import concourse.bass as bass
import concourse.mybir as mybir
from concourse.bass_utils import run_bass_kernel_spmd

F32 = mybir.dt.float32
BF16 = mybir.dt.bfloat16
I32 = mybir.dt.int32
ALU = mybir.AluOpType
AF = mybir.ActivationFunctionType

D = 1024
T_LAT = 4096
T_CTX = 256
T = T_LAT + T_CTX
DEPTH = 4
DFF = 2816
NF = DFF // 128
TT = 256
NT = T // TT
NS = T // 128
EPS = 1e-6
NEG = -30000.0


class Buf:
    __slots__ = ("name", "writer", "readers")

    def __init__(self, name):
        self.name = name
        self.writer = None
        self.readers = []


class _Ins:
    __slots__ = ("fn", "signal", "waits", "dma")

    def __init__(self, fn, waits, dma=None):
        self.fn = fn
        self.signal = False
        self.waits = waits
        self.dma = dma


class EngQ:
    def __init__(self, name, sem):
        self.name = name
        self.sem = sem
        self.ins = []
        self.sigidx = []
        self.seen = {}
        self.flushed = 0
        self.pending = []

    def resolve(self, idx):
        j = bisect.bisect_left(self.sigidx, idx)
        if j < len(self.sigidx):
            return j + 1
        last = len(self.ins) - 1
        while self.ins[last].dma is not None or self.ins[last].fn is None:
            last -= 1
        assert last >= idx and last >= self.flushed, (self.name, idx, last, self.flushed)
        self.ins[last].signal = True
        self.sigidx.append(last)
        return len(self.sigidx)


class Sched:
    NDMA_SEMS = 8

    def __init__(self, nc, stack):
        self.nc = nc
        self.q = {}
        for name in ("pe", "act", "dve", "pool", "sp"):
            sem = stack.enter_context(nc.semaphore("s_" + name))
            self.q[name] = EngQ(name, sem)
        self.dsem = {}
        self.dcnt = {}
        for name in ("act", "pool", "sp"):
            self.dsem[name] = [stack.enter_context(nc.semaphore("d_%s%d" % (name, i)))
                               for i in range(self.NDMA_SEMS)]
            self.dcnt[name] = 0
        self.semval = {}
        self.semobj = {}

    def _need(self, q, tok, waits):
        if tok is None:
            return
        if tok[0] == "e":
            _, pq, idx = tok
            if pq is q and q.name == "pe":
                return
            sem = pq.sem
            if idx < pq.flushed and q.seen.get(id(sem), 0) >= len([0 for s in pq.sigidx if s < pq.flushed]):
                pass
            val = pq.resolve(idx)
        else:
            _, sem, val = tok
        key = id(sem)
        if q.seen.get(key, 0) >= val:
            return
        q.seen[key] = val
        waits.append((sem, val))

    def _deps(self, q, reads, writes):
        waits = []
        for b in reads:
            self._need(q, b.writer, waits)
        for b in writes:
            self._need(q, b.writer, waits)
            for r in b.readers:
                self._need(q, r, waits)
        return waits

    def _commit(self, tok, reads, writes):
        for b in reads:
            b.readers.append(tok)
            if len(b.readers) > 64:
                b.readers = b.readers[-64:]
        for b in writes:
            b.writer = tok
            b.readers = []

    cut = None
    nrec = 0

    def op(self, eng, fn, reads=(), writes=(), signal=False):
        self.nrec += 1
        if self.cut is not None and self.nrec > self.cut:
            return None
        if self.cut is not None and self.nrec >= self.cut - 2:
            print("REC", self.nrec, eng, [b.name for b in reads], [b.name for b in writes])
        q = self.q[eng]
        waits = q.pending + self._deps(q, reads, writes)
        q.pending = []
        ins = _Ins(fn, waits)
        if self.cut is not None and self.nrec >= self.cut - 2:
            print("   WAITS", [(str(sm), v) for sm, v in waits])
        q.ins.append(ins)
        idx = len(q.ins) - 1
        if signal:
            ins.signal = True
            q.sigidx.append(idx)
        tok = ("e", q, idx)
        self._commit(tok, reads, writes)
        return tok

    def dma(self, eng, fns, reads=(), writes=()):
        self.nrec += 1
        if self.cut is not None and self.nrec > self.cut:
            return None
        q = self.q[eng]
        waits = q.pending + self._deps(q, reads, writes)
        q.pending = []
        n = self.dcnt[eng]
        self.dcnt[eng] += 1
        pool = self.dsem[eng]
        sem = pool[n % len(pool)]
        key = id(sem)
        self.semobj[key] = sem
        base = self.semval.get(key, 0)
        if base > 0 and q.seen.get(key, 0) < base:
            q.seen[key] = base
            waits.append((sem, base))
        val = base + 16 * len(fns)
        self.semval[key] = val
        if self.cut is not None:
            print("DMA", self.nrec, eng, n, str(sem)[21:40], "base", base, "val", val, [b.name for b in writes], "waits", [(str(sm)[21:40], v) for sm, v in waits])
        first = True
        for fn in fns:
            q.ins.append(_Ins(fn, waits if first else [], dma=sem))
            first = False
        tok = ("d", sem, val)
        self._commit(tok, reads, writes)
        return tok

    def barrier(self):
        import os
        if os.environ.get("NOBAR"):
            return
        toks = []
        for p in self.q.values():
            j = len(p.ins) - 1
            while j >= p.flushed and (p.ins[j].dma is not None or p.ins[j].fn is None):
                j -= 1
            if j >= p.flushed:
                toks.append(("e", p, j))
        for key, val in self.semval.items():
            toks.append(("d", self.semobj[key], val))
        for q in self.q.values():
            waits = []
            for t in toks:
                if t[0] == "e" and t[1] is q:
                    continue
                self._need(q, t, waits)
            q.pending = q.pending + waits

    def _replay(self, name, e):
        q = self.q[name]
        for ins in q.ins[q.flushed:]:
            for sem, val in ins.waits:
                e.wait_ge(sem, val)
            if ins.fn is None:
                continue
            r = ins.fn(e)
            if ins.dma is not None:
                r.then_inc(ins.dma, 16)
            elif ins.signal:
                r.then_inc(q.sem, 1)
        q.flushed = len(q.ins)

    def finish(self):
        q = self.q["sp"]
        if q.pending:
            q.ins.append(_Ins(None, q.pending))
            q.pending = []
        self.flush()

    def flush(self):
        nc = self.nc
        with nc.Block() as block:
            @block.tensor
            def _(e):
                self._replay("pe", e)

            @block.scalar
            def _(e):
                self._replay("act", e)

            @block.vector
            def _(e):
                self._replay("dve", e)

            @block.gpsimd
            def _(e):
                self._replay("pool", e)

            @block.sync
            def _(e):
                self._replay("sp", e)


def _hg_consts():
    r = np.arange(128)
    same = (r[:, None] // 64) == (r[None, :] // 64)
    mid = (r // 64) * 64 + 31
    mqf = np.zeros((128, 132), np.float32)
    mqb = np.zeros((128, 132), np.float32)
    for t in range(128):
        m = mid[t]
        for rr in range(128):
            if not same[rr, t]:
                continue
            a = (1.0 if rr <= t else 0.0) - (1.0 if rr <= m else 0.0)
            mqf[rr, t] = a
            b = (1.0 if rr >= t else 0.0) - (1.0 if rr >= m else 0.0)
            mqb[rr, t] = b
    for c in range(2):
        mqf[c * 64:(c + 1) * 64, 128 + c] = 1.0
        mqb[c * 64:(c + 1) * 64, 128 + c] = 1.0
        mqf[c * 64:c * 64 + 32, 130 + c] = 1.0
        mqb[c * 64 + 31:(c + 1) * 64, 130 + c] = 1.0
    mkf = (same & (r[:, None] > r[None, :])).astype(np.float32)
    mkb = (same & (r[:, None] < r[None, :])).astype(np.float32)
    mf = (same & (r[:, None] <= r[None, :])).astype(np.float32)
    mb = (same & (r[:, None] >= r[None, :])).astype(np.float32)
    maskf = np.stack([mf, mf], axis=1)
    maskb = np.stack([mb, mb], axis=1)
    return mqf, mqb, mkf, mkb, maskf, maskb


def _rope_tables():
    t = np.arange(T_LAT)
    pos = np.stack([t // 64, t % 64], -1).astype(np.float32)
    nf = 16
    inv = (10000.0 ** (-np.arange(nf, dtype=np.float32) / nf)).astype(np.float32)
    ang = pos[:, :, None] * inv
    cos, sin = np.cos(ang), np.sin(ang)
    ct = np.zeros((64, T_LAT), np.float32)
    st = np.zeros((64, T_LAT), np.float32)
    for a in range(2):
        ct[a * 32:a * 32 + 16] = cos[:, a, :].T
        ct[a * 32 + 16:a * 32 + 32] = cos[:, a, :].T
        st[a * 32:a * 32 + 16] = -sin[:, a, :].T
        st[a * 32 + 16:a * 32 + 32] = sin[:, a, :].T
    ct = np.concatenate([ct, ct], 0)
    st = np.concatenate([st, st], 0)
    P = np.zeros((128, 128), np.float32)
    for m in range(128):
        k = m + 16 if (m % 32) < 16 else m - 16
        P[k, m] = 1.0
    return ct, st, P


def _na_tile_kinds():
    kinds = {}
    reps = []
    per_q = []
    for m in range(32):
        rows = []
        for qr in (2 * m, 2 * m + 1):
            r0 = min(max(qr - 4, 0), 56)
            rows += list(range(r0, r0 + 8))
        tiles = sorted(set(rr // 2 for rr in rows))
        lst = []
        for n in tiles:
            if 2 <= m <= 29:
                key = ("i", n - m)
            else:
                key = ("b", m, n)
            if key not in kinds:
                kinds[key] = len(reps)
                reps.append((m, n))
            lst.append((n, kinds[key]))
        per_q.append(lst)
    return per_q, reps


NA_PER_Q, NA_REPS = _na_tile_kinds()
NKIND = len(NA_REPS)


def _na_bias_host(rpb_l):
    out = np.full((NKIND, 128, 6, 128), NEG, np.float32)
    kk = np.arange(128)
    kr_l, kc = kk // 64, kk % 64
    for ki, (m, n) in enumerate(NA_REPS):
        qr = 2 * m + kr_l
        qc = kc
        krow = 2 * n + kr_l
        r0 = np.clip(qr - 4, 0, 56)
        c0 = np.clip(qc - 8, 0, 48)
        row_ok = (krow[:, None] >= r0[None, :]) & (krow[:, None] < r0[None, :] + 8)
        col_ok = (kc[:, None] >= c0[None, :]) & (kc[:, None] < c0[None, :] + 16)
        d_row = krow[:, None] - qr[None, :]
        d_col = np.clip(kc[:, None] - qc[None, :], -15, 15)
        ok = row_ok & col_ok
        di = np.clip(d_row + 7, 0, 14)
        for h in range(6):
            vals = rpb_l[h][di, d_col + 15]
            out[ki, :, h, :] = np.where(ok, vals, NEG)
    return out


def build(depth=DEPTH, dbg=None):
    nc = bass.Bass("TRN2", target_bir_lowering=False)
    dt = nc.dram_tensor

    def din(name, shape, dtype=F32):
        return dt(name, list(shape), dtype, kind="ExternalInput").ap()

    xin = din("xin", [128, 8, T])
    cc = din("cc", [128, 8, 2])
    adab2 = din("adab2", [128, DEPTH, 144])
    normg2 = din("normg2", [128, DEPTH, 3, 16])
    finalg = din("finalg", [128, 8])
    ada_w = din("ada_w", [depth, D, 9 * D])
    w1 = din("ffn_w1", [depth, 2, D, DFF])
    w3 = din("ffn_w3", [depth, 2, D, DFF])
    w2 = din("ffn_w2", [depth, 2, DFF, D])
    w_in = din("w_in", [depth, D, 3072])
    w_out = din("w_out", [depth, D, D])
    lbrow = din("lbrow", [128, DEPTH * 512])
    lbcol = din("lbcol", [128, DEPTH * 4])
    hgng = din("hgng", [128, DEPTH * 2])
    nabias = din("nabias", [depth, NKIND, 128, 6 * 128])
    sinkrep = din("sinkrep", [128, DEPTH * 6])
    mqf_d = din("mqf", [128, 132]); mqb_d = din("mqb", [128, 132])
    mkf_d = din("mkf", [128, 128]); mkb_d = din("mkb", [128, 128])
    maskf_d = din("maskf", [128, 256]); maskb_d = din("maskb", [128, 256])
    ropec_d = din("ropec", [128, T_LAT]); ropes_d = din("ropes", [128, T_LAT])
    swapm_d = din("swapm", [128, 128])
    swm_d = din("swmask", [128, 2, 256])
    ident_d = din("ident", [128, 128])
    bones_d = din("bones", [128, 128])

    out = dt("out", [128, 8, T_LAT], F32, kind="ExternalOutput").ap()

    xs = (dt("xs", [128, 8, T], F32, kind="ExternalOutput") if dbg else dt("xs", [128, 8, T], F32)).ap()
    dbgmod = dt("dbgmod", [128, DEPTH * 144 + DEPTH * 48 * 2], F32, kind="ExternalOutput").ap() if dbg else None
    _dt = dt
    if dbg:
        def dt(name, shape, dtype, kind=None):
            return _dt(name, shape, dtype, kind="ExternalOutput")
    naq = dt("naq", [128, 3, T], BF16).ap(); nak = dt("nak", [128, 3, T], BF16).ap()
    swq = dt("swq", [128, 3, T], BF16).ap(); swk = dt("swk", [128, 3, T], BF16).ap()
    nav = dt("nav", [128, NS, 384], BF16).ap(); swv = dt("swv", [128, NS, 384], BF16).ap()
    hqt = [dt("hqt%d" % d_, [128, 2, T], BF16).ap() for d_ in range(2)]
    hkt = [dt("hkt%d" % d_, [128, 2, T], BF16).ap() for d_ in range(2)]
    hkh = dt("hkh", [128, NS, 512], BF16).ap()
    hv = dt("hv", [128, NS, 256], BF16).ap()
    hgt = dt("hgt", [128, 2, T], BF16).ap()
    omix = dt("omix", [128, 8, T], BF16).ap()

    dbg_out = {}

    with contextlib.ExitStack() as top:
        S = Sched(nc, top)
        import os
        if os.environ.get("CUT"):
            S.cut = int(os.environ["CUT"])
        cnt = [0]

        def sb(name, shape, dtype=F32, st=top):
            cnt[0] += 1
            return st.enter_context(nc.sbuf_tensor("%s_%d" % (name, cnt[0]), list(shape), dtype))

        def pst(name, st):
            cnt[0] += 1
            return st.enter_context(nc.psum_tensor("%s_%d" % (name, cnt[0]), [128, 512], F32))

        MOD = sb("MOD", [128, DEPTH, 144])
        GM = sb("GM", [128, DEPTH, 3, 16])
        GT = sb("GT", [128, DEPTH, 3, 16])
        NG2 = sb("NG2", [128, DEPTH, 3, 16])
        FG = sb("FG", [128, 8])
        ones32 = sb("ones32", [128, 128])
        onesb = sb("onesb", [128, 128], BF16)
        identb = sb("identb", [128, 128], BF16)
        bones = sb("bones32", [128, 128])
        EXPL = sb("EXPL", [128, 2, 2, 2 * NS])
        EMID = sb("EMID", [128, 2, 2, 2 * NS])
        LBC = sb("LBC", [128, DEPTH * 4])
        OMLC = sb("OMLC", [128, DEPTH * 4])
        OMBC = sb("OMBC", [128, DEPTH * 4])
        HGN = sb("HGN", [128, DEPTH * 2])
        SINKE = sb("SINKE", [128, DEPTH * 6])
        B = {}

        def buf(name):
            if name not in B:
                B[name] = Buf(name)
            return B[name]

        with contextlib.ExitStack() as ph:
            scv = sb("scv", [128, 8, 2], st=ph)
            adb = sb("adb", [128, DEPTH, 144], st=ph)
            awb = [sb("awb%d" % i, [128, 8, 1024], st=ph) for i in range(2)]
            lbr = sb("lbr", [128, DEPTH * 4], st=ph)
            lbe = sb("lbe", [128, DEPTH * 4], st=ph)
            lbs = sb("lbs", [128, 4], st=ph)
            pm = [pst("pm%d" % i, ph) for i in range(2)]
            S.dma("sp", [lambda e: e.dma_start(out=scv[:], in_=cc)], writes=[buf("scv")])
            S.dma("sp", [lambda e: e.dma_start(out=adb[:], in_=adab2)], writes=[buf("adb")])
            S.dma("sp", [lambda e: e.dma_start(out=NG2[:], in_=normg2)], writes=[buf("NG2")])
            S.dma("sp", [lambda e: e.dma_start(out=FG[:], in_=finalg)], writes=[buf("FG")])
            S.dma("sp", [lambda e: e.dma_start(out=identb[:], in_=ident_d)], writes=[buf("identb")]) if False else None
            S.dma("sp", [lambda e: e.dma_start(out=awb[0][:, 0, 0:128], in_=ident_d)], writes=[buf("awb0")])
            S.op("dve", lambda e: e.tensor_copy(out=identb[:], in_=awb[0][:, 0, 0:128]), reads=[buf("awb0")], writes=[buf("identb")])
            S.dma("sp", [lambda e: e.dma_start(out=bones[:], in_=bones_d)], writes=[buf("bones")])
            S.dma("sp", [lambda e: e.dma_start(out=lbr[:], in_=lbcol)], writes=[buf("lbr")])
            S.dma("sp", [lambda e: e.dma_start(out=HGN[:], in_=hgng)], writes=[buf("HGN")])
            S.dma("sp", [lambda e: e.dma_start(out=SINKE[:], in_=sinkrep)], writes=[buf("SINKE")])
            S.op("dve", lambda e: e.memset(ones32[:], 1.0), writes=[buf("ones32")])
            S.op("dve", lambda e: e.memset(onesb[:], 1.0), writes=[buf("onesb")])
            S.op("act", lambda e: e.activation(out=SINKE[:], in_=SINKE[:], func=AF.Exp), reads=[buf("SINKE")], writes=[buf("SINKE")])
            S.op("act", lambda e: e.activation(out=scv[:], in_=scv[:], func=AF.Silu), reads=[buf("scv")], writes=[buf("scv")])
            S.op("act", lambda e: e.activation(out=lbe[:], in_=lbr[:], func=AF.Exp), reads=[buf("lbr")], writes=[buf("lbe")])
            S.op("dve", lambda e: e.tensor_tensor(out=lbs[:], in0=lbe[:, 0:4], in1=lbe[:, 4:8], op=ALU.add), reads=[buf("lbe")], writes=[buf("lbs")])
            S.op("dve", lambda e: e.tensor_tensor(out=lbs[:], in0=lbs[:], in1=lbe[:, 8:12], op=ALU.add), reads=[buf("lbe"), buf("lbs")], writes=[buf("lbs")])
            S.op("dve", lambda e: e.tensor_tensor(out=lbs[:], in0=lbs[:], in1=lbe[:, 12:16], op=ALU.add), reads=[buf("lbe"), buf("lbs")], writes=[buf("lbs")])
            S.op("dve", lambda e: e.reciprocal(out=lbs[:], in_=lbs[:]), reads=[buf("lbs")], writes=[buf("lbs")])
            S.op("dve", lambda e: e.memset(LBC[:, 0:4], 0.0), writes=[buf("LBC")])
            for l in range(1, DEPTH):
                S.op("dve", lambda e, l=l: e.tensor_tensor(out=lbe[:, 4 * l:4 * l + 4], in0=lbe[:, 4 * l:4 * l + 4], in1=lbs[:], op=ALU.mult),
                     reads=[buf("lbs"), buf("lbe")], writes=[buf("lbe")])
                S.op("dve", lambda e, l=l: e.tensor_tensor(out=LBC[:, 4 * l:4 * l + 4], in0=LBC[:, 4 * l - 4:4 * l], in1=lbe[:, 4 * l:4 * l + 4], op=ALU.add),
                     reads=[buf("lbe"), buf("LBC")], writes=[buf("LBC")])
            S.op("dve", lambda e: e.tensor_scalar(out=OMBC[:], in0=LBC[:], scalar1=-1.0, scalar2=1.0, op0=ALU.mult, op1=ALU.add),
                 reads=[buf("LBC")], writes=[buf("OMBC")])
            S.op("dve", lambda e: e.tensor_scalar(out=OMLC[:], in0=OMBC[:], scalar1=-1.0, scalar2=None, op0=ALU.mult),
                 reads=[buf("OMBC")], writes=[buf("OMLC")])
            blk = 0
            for l in range(depth):
                for j in range(9):
                    ab = awb[blk % 2]
                    bn = "awb%d" % (blk % 2)
                    src = ada_w[l, :, j * 1024:(j + 1) * 1024].rearrange("(c p) n -> p c n", p=128)
                    S.dma("sp", [lambda e, ab=ab, src=src: e.dma_start(out=ab[:], in_=src)], writes=[buf(bn)])
                    for ch in range(8):
                        col = j * 16 + ch * 2
                        for k in range(8):
                            S.op("pe", lambda e, ab=ab, ch=ch, k=k, col=col, l=l: e.matmul(
                                pm[l % 2][:, col:col + 2], lhsT=ab[:, k, ch * 128:(ch + 1) * 128], rhs=scv[:, k, :],
                                start=(k == 0), stop=(k == 7)),
                                reads=[buf(bn), buf("scv")], writes=[buf("pm%d" % (l % 2))])
                    blk += 1
                S.op("dve", lambda e, l=l: e.tensor_tensor(out=MOD[:, l, :], in0=pm[l % 2][:, 0:144], in1=adb[:, l, :], op=ALU.add),
                     reads=[buf("pm%d" % (l % 2)), buf("adb")], writes=[buf("MOD")])
                for s in range(3):
                    S.op("dve", lambda e, l=l, s=s: e.scalar_tensor_tensor(
                        out=GM[:, l, s, :], in0=MOD[:, l, (3 * s + 1) * 16:(3 * s + 2) * 16], scalar=1.0, in1=NG2[:, l, s, :],
                        op0=ALU.add, op1=ALU.mult), reads=[buf("MOD"), buf("NG2")], writes=[buf("GM")])
                    S.op("dve", lambda e, l=l, s=s: e.tensor_scalar(
                        out=GT[:, l, s, :], in0=MOD[:, l, (3 * s + 2) * 16:(3 * s + 3) * 16], scalar1=(1.0 if s == 1 else 0.5), scalar2=None,
                        op0=ALU.mult), reads=[buf("MOD")], writes=[buf("GT")])
            if dbg:
                S.dma("sp", [lambda e: e.dma_start(out=dbgmod[:, 0:DEPTH * 144], in_=MOD[:].rearrange("p a b -> p (a b)"))], reads=[buf("MOD")])
                S.dma("sp", [lambda e: e.dma_start(out=dbgmod[:, DEPTH * 144:DEPTH * 144 + DEPTH * 48], in_=GM[:].rearrange("p a b c -> p (a b c)"))], reads=[buf("GM")])
                S.dma("sp", [lambda e: e.dma_start(out=dbgmod[:, DEPTH * 144 + DEPTH * 48:], in_=GT[:].rearrange("p a b c -> p (a b c)"))], reads=[buf("GT")])
            S.barrier()
            S.flush()

        def gm_ap(l, s, ch, v):
            return GM[:, l, s, ch * 2 + v:ch * 2 + v + 1]

        def sh_ap(l, s, ch, v):
            i = (3 * s) * 16 + ch * 2 + v
            return MOD[:, l, i:i + 1]

        def gt_ap(l, s, ch, v):
            return GT[:, l, s, ch * 2 + v:ch * 2 + v + 1]

        RD = [buf("GM"), buf("MOD"), buf("GT")]

        def norm_mod(xt, xtn, l, s, v, hdst, hname, W, ps_ss, ssname, sq, rs, tmp8, gmul=True):
            S.op("act", lambda e: e.activation(out=sq[:, :, 0:W], in_=xt[:, :, 0:W], func=AF.Square), reads=[buf(xtn)], writes=[buf("sq")] + [buf("tmp8_%d" % c_) for c_ in range(8)])
            for c in range(8):
                S.op("pe", lambda e, c=c: e.matmul(ps_ss[:, 0:W], lhsT=ones32[:], rhs=sq[:, c, 0:W], start=(c == 0), stop=(c == 7)),
                     reads=[buf("sq"), buf("ones32")], writes=[buf(ssname)], signal=(c == 7))
            S.op("act", lambda e: e.activation(out=rs[:, 0:W], in_=ps_ss[:, 0:W], func=AF.Sqrt, scale=1.0 / D, bias=epsb[:]),
                 reads=[buf(ssname), buf("epsb")], writes=[buf("rs")])
            S.op("dve", lambda e: e.reciprocal(out=rs[:, 0:W], in_=rs[:, 0:W]), reads=[buf("rs")], writes=[buf("rs")])
            for c in range(8):
                S.op("dve", lambda e, c=c: e.tensor_tensor(out=tmp8[:, c, 0:W], in0=xt[:, c, 0:W], in1=rs[:, 0:W], op=ALU.mult),
                     reads=[buf(xtn), buf("rs")], writes=[buf("tmp8_%d" % c)] + ([buf("sq")] if c == 0 else []))
                if gmul:
                    S.op("act", lambda e, c=c: e.activation(out=hdst[:, c, 0:W], in_=tmp8[:, c, 0:W], func=AF.Identity,
                                                            scale=gm_ap(l, s, c, v), bias=sh_ap(l, s, c, v)),
                         reads=[buf("tmp8_%d" % c)] + RD, writes=[buf(hname)])
                else:
                    S.op("act", lambda e, c=c: e.activation(out=hdst[:, c, 0:W], in_=tmp8[:, c, 0:W], func=AF.Identity,
                                                            scale=FG[:, c:c + 1]),
                         reads=[buf("tmp8_%d" % c), buf("FG")], writes=[buf(hname)])

        epsb = sb("epsb", [128, 1])
        S.op("dve", lambda e: e.memset(epsb[:], EPS), writes=[buf("epsb")])

        def tile_v(i):
            return 1 if i == NT - 1 else 0

        def load_ffn(l, which, W1, W3, W2):
            fns = []
            for c in range(8):
                for (dst, srcw) in ((W1, w1), (W3, w3)):
                    for hlf in range(2):
                        fns.append(lambda e, c=c, dst=dst, srcw=srcw, hlf=hlf: e.dma_start(
                            out=dst[:, c, hlf * 1408:(hlf + 1) * 1408],
                            in_=srcw[l, which, c * 128:(c + 1) * 128, hlf * 1408:(hlf + 1) * 1408]))
            S.dma("pool", fns, writes=[buf("W1"), buf("W3")])
            fns = []
            for f in range(NF):
                fns.append(lambda e, f=f: e.dma_start(out=W2[:, f, :], in_=w2[l, which, f * 128:(f + 1) * 128, :]))
            S.dma("pool", fns, writes=[buf("W2")])

        def ffn_u(h, hname, W1, W3, pu, s1, g, W):
            for f in range(NF):
                p = pu[f % 2]
                pn = "pu%d" % (f % 2)
                for (wt, wn, off) in ((W1, "W1", 0), (W3, "W3", 256)):
                    for k in range(8):
                        S.op("pe", lambda e, p=p, wt=wt, f=f, k=k, off=off: e.matmul(
                            p[:, off:off + W], lhsT=wt[:, k, f * 128:(f + 1) * 128], rhs=h[:, k, 0:W], start=(k == 0), stop=(k == 7)),
                            reads=[buf(wn), buf(hname)], writes=[buf(pn)], signal=(k == 7 and off == 256))
                sb_ = s1[f % 2]
                sn = "s1_%d" % (f % 2)
                S.op("act", lambda e, p=p, sb_=sb_: e.activation(out=sb_[:, 0:W], in_=p[:, 0:W], func=AF.Silu), reads=[buf(pn)], writes=[buf(sn)])
                S.op("dve", lambda e, p=p, sb_=sb_, f=f: e.tensor_tensor(out=g[:, f, 0:W], in0=sb_[:, 0:W], in1=p[:, 256:256 + W], op=ALU.mult),
                     reads=[buf(sn), buf(pn)], writes=[buf("g")])

        def ffn_y(xt, xtn, W2, py, g, l, s, v, W):
            for d in range(8):
                p = py[d % 2]
                pn = "py%d" % (d % 2)
                for f in range(NF):
                    S.op("pe", lambda e, p=p, d=d, f=f: e.matmul(p[:, 0:W], lhsT=W2[:, f, d * 128:(d + 1) * 128], rhs=g[:, f, 0:W],
                                                               start=(f == 0), stop=(f == NF - 1)),
                         reads=[buf("W2"), buf("g")], writes=[buf(pn)], signal=(f == NF - 1))
                S.op("dve", lambda e, p=p, d=d: e.scalar_tensor_tensor(out=xt[:, d, 0:W], in0=p[:, 0:W], scalar=gt_ap(l, s, d, v),
                                                                     in1=xt[:, d, 0:W], op0=ALU.mult, op1=ALU.add),
                     reads=[buf(pn), buf(xtn)] + RD, writes=[buf(xtn)])

        def phase_A(l):
            src = xin if l == 0 else xs
            with contextlib.ExitStack() as ph:
                W1 = sb("W1", [128, 8, DFF], BF16, ph); W3 = sb("W3", [128, 8, DFF], BF16, ph)
                W2 = sb("W2", [128, NF, D], BF16, ph)
                xb = [sb("xb%d" % i, [128, 8, TT], F32, ph) for i in range(2)]
                hb = [sb("hb%d" % i, [128, 8, TT], BF16, ph) for i in range(2)]
                tmp8 = sb("tmp8", [128, 8, TT], F32, ph); sq = tmp8
                rs = sb("rs", [128, TT], F32, ph)
                s1 = [sb("s1_%d" % i, [128, TT], F32, ph) for i in range(2)]
                g = sb("g", [128, NF, TT], BF16, ph)
                pss = pst("pss", ph)
                pu = [pst("pu%d" % i, ph) for i in range(2)]
                py = [pst("py%d" % i, ph) for i in range(2)]
                load_ffn(l, 0, W1, W3, W2)

                def load(i):
                    S.dma("sp", [lambda e: e.dma_start(out=xb[i % 2][:], in_=src[:, :, i * TT:(i + 1) * TT])], writes=[buf("xb%d" % (i % 2))])

                def nm(i):
                    norm_mod(xb[i % 2], "xb%d" % (i % 2), l, 0, tile_v(i), hb[i % 2], "hb%d" % (i % 2), TT, pss, "pss", sq, rs, tmp8)

                load(0)
                nm(0)
                for i in range(NT):
                    if i + 1 < NT:
                        load(i + 1)
                    ffn_u(hb[i % 2], "hb%d" % (i % 2), W1, W3, pu, s1, g, TT)
                    if i + 1 < NT:
                        nm(i + 1)
                    ffn_y(xb[i % 2], "xb%d" % (i % 2), W2, py, g, l, 0, tile_v(i), TT)
                    S.dma("sp", [lambda e, i=i: e.dma_start(out=xs[:, :, i * TT:(i + 1) * TT], in_=xb[i % 2][:])], reads=[buf("xb%d" % (i % 2))], writes=[buf("xs_w")])
                S.barrier()
                S.flush()

        def phase_D(l, last):
            ntiles = NT - 1 if last else NT
            with contextlib.ExitStack() as ph:
                W1 = sb("W1", [128, 8, DFF], BF16, ph); W3 = sb("W3", [128, 8, DFF], BF16, ph)
                W2 = sb("W2", [128, NF, D], BF16, ph)
                WO = sb("WO", [128, 8, D], BF16, ph)
                xb = [sb("xb%d" % i, [128, 8, TT], F32, ph) for i in range(2)]
                ob = [sb("ob%d" % i, [128, 8, TT], BF16, ph) for i in range(2)]
                hb = [sb("hb%d" % i, [128, 8, TT], BF16, ph) for i in range(1)]
                tmp8 = sb("tmp8", [128, 8, TT], F32, ph); sq = tmp8
                rs = sb("rs", [128, TT], F32, ph)
                s1 = [sb("s1_%d" % i, [128, TT], F32, ph) for i in range(2)]
                g = sb("g", [128, NF, TT], BF16, ph)
                pss = pst("pss", ph)
                pu = [pst("pu%d" % i, ph) for i in range(2)]
                py = [pst("py%d" % i, ph) for i in range(2)]
                load_ffn(l, 1, W1, W3, W2)
                S.dma("pool", [lambda e, c=c: e.dma_start(out=WO[:, c, :], in_=w_out[l, c * 128:(c + 1) * 128, :]) for c in range(8)], writes=[buf("WO")])

                def load(i):
                    S.dma("sp", [lambda e: e.dma_start(out=xb[i % 2][:], in_=xs[:, :, i * TT:(i + 1) * TT])], writes=[buf("xb%d" % (i % 2))])
                    if dbg == "AD":
                        S.op("dve", lambda e: e.memset(ob[i % 2][:], 0.0), writes=[buf("ob%d" % (i % 2))])
                    else:
                        S.dma("sp", [lambda e: e.dma_start(out=ob[i % 2][:], in_=omix[:, :, i * TT:(i + 1) * TT])], writes=[buf("ob%d" % (i % 2))])

                load(0)
                for i in range(ntiles):
                    v = tile_v(i)
                    xt = xb[i % 2]; xtn = "xb%d" % (i % 2)
                    if i + 1 < ntiles:
                        load(i + 1)
                    for d in range(8):
                        p = py[d % 2]; pn = "py%d" % (d % 2)
                        for k in range(8):
                            S.op("pe", lambda e, p=p, d=d, k=k, i=i: e.matmul(p[:, 0:TT], lhsT=WO[:, k, d * 128:(d + 1) * 128], rhs=ob[i % 2][:, k, :],
                                                                           start=(k == 0), stop=(k == 7)),
                                 reads=[buf("WO"), buf("ob%d" % (i % 2))], writes=[buf(pn)], signal=(k == 7))
                        S.op("dve", lambda e, p=p, d=d, xt=xt, v=v: e.scalar_tensor_tensor(out=xt[:, d, :], in0=p[:, 0:TT], scalar=gt_ap(l, 1, d, v),
                                                                                       in1=xt[:, d, :], op0=ALU.mult, op1=ALU.add),
                             reads=[buf(pn), buf(xtn)] + RD, writes=[buf(xtn)])
                    norm_mod(xt, xtn, l, 2, v, hb[0], "hb0", TT, pss, "pss", sq, rs, tmp8)
                    ffn_u(hb[0], "hb0", W1, W3, pu, s1, g, TT)
                    ffn_y(xt, xtn, W2, py, g, l, 2, v, TT)
                    if last:
                        S.op("act", lambda e, xt=xt: e.activation(out=sq[:], in_=xt[:], func=AF.Square), reads=[buf(xtn)], writes=[buf("sq")] + [buf("tmp8_%d" % c_) for c_ in range(8)])
                        for c in range(8):
                            S.op("pe", lambda e, c=c: e.matmul(pss[:, 0:TT], lhsT=ones32[:], rhs=sq[:, c, :], start=(c == 0), stop=(c == 7)),
                                 reads=[buf("sq"), buf("ones32")], writes=[buf("pss")], signal=(c == 7))
                        S.op("act", lambda e: e.activation(out=rs[:], in_=pss[:, 0:TT], func=AF.Sqrt, scale=1.0 / D, bias=epsb[:]),
                             reads=[buf("pss"), buf("epsb")], writes=[buf("rs")])
                        S.op("dve", lambda e: e.reciprocal(out=rs[:], in_=rs[:]), reads=[buf("rs")], writes=[buf("rs")])
                        for c in range(8):
                            S.op("dve", lambda e, c=c, xt=xt: e.scalar_tensor_tensor(out=xt[:, c, :], in0=xt[:, c, :], scalar=FG[:, c:c + 1], in1=rs[:],
                                                                                 op0=ALU.mult, op1=ALU.mult),
                                 reads=[buf(xtn), buf("rs"), buf("FG")], writes=[buf(xtn)])
                        S.dma("sp", [lambda e, i=i, xt=xt: e.dma_start(out=out[:, :, i * TT:(i + 1) * TT], in_=xt[:])], reads=[buf(xtn)], writes=[buf("out")])
                    else:
                        S.dma("sp", [lambda e, i=i, xt=xt: e.dma_start(out=xs[:, :, i * TT:(i + 1) * TT], in_=xt[:])], reads=[buf(xtn)], writes=[buf("xs_w")])
                S.barrier()
                S.flush()


        def phase_B(l):
            import os
            if os.environ.get("CUTB"):
                S.cut = S.nrec + int(os.environ["CUTB"])
            with contextlib.ExitStack() as ph:
                t1 = sb("t1", [128, TT], F32, ph); t2 = sb("t2", [128, TT], F32, ph)
                x32 = sb("x32", [128, TT], F32, ph); sw32 = sb("sw32", [128, TT], F32, ph)
                WIN = sb("WIN", [128, 8, 3072], BF16, ph)
                WKD = sb("WKD", [128, 8, 384], BF16, ph); WVD = sb("WVD", [128, 8, 384], BF16, ph)
                xb = [sb("xb%d" % i, [128, 8, TT], F32, ph) for i in range(2)]
                hx = [sb("hx%d" % i, [128, 8, TT], BF16, ph) for i in range(2)]
                tmp8 = sb("tmp8", [128, 8, TT], F32, ph); sq = tmp8
                rs = sb("rs", [128, TT], F32, ph)
                lbe = sb("lbe", [128, DEPTH * 512], F32, ph)
                lbs = sb("lbs", [128, 512], F32, ph)
                LBROW = sb("LBROW", [128, 512], F32, ph); OMLROW = sb("OMLROW", [128, 512], F32, ph)
                MQ = [sb("MQ%d" % d_, [128, 132], F32, ph) for d_ in range(2)]
                MK = [sb("MK%d" % d_, [128, 128], F32, ph) for d_ in range(2)]
                SWP = sb("SWP", [128, 128], BF16, ph); swp32 = sb("swp32", [128, 128], F32, ph)
                RC = sb("RC", [128, T_LAT], F32, ph); RSN = sb("RSN", [128, T_LAT], F32, ph)
                qs = [sb("qs%d" % p_, [128, TT], F32, ph) for p_ in range(2)]
                kts = [[sb("kts%d%d" % (d_, p_), [128, TT], F32, ph) for p_ in range(2)] for d_ in range(2)]
                sig = sb("sig", [128, 512], F32, ph); LOGF = sb("LOGF", [128, 512], F32, ph); kk = sb("kk", [128, 512], F32, ph)
                ek = sb("ek", [128, 512], F32, ph)
                e1 = [sb("e1_%d" % i, [128, 128], F32, ph) for i in range(2)]
                xb16 = sb("xb16", [128, TT], BF16, ph)
                fsig = sb("fsig", [128, TT], F32, ph)
                st = {}
                for i in range(2):
                    st["naq", i] = sb("snaq%d" % i, [128, 3, TT], BF16, ph); st["nak", i] = sb("snak%d" % i, [128, 3, TT], BF16, ph)
                    st["swq", i] = sb("sswq%d" % i, [128, 3, TT], BF16, ph); st["swk", i] = sb("sswk%d" % i, [128, 3, TT], BF16, ph)
                    st["hgt", i] = sb("shgt%d" % i, [128, 2, TT], BF16, ph)
                    for d_ in range(2):
                        st["hqt%d" % d_, i] = sb("shqt%d%d" % (d_, i), [128, 2, TT], BF16, ph)
                        st["hkt%d" % d_, i] = sb("shkt%d%d" % (d_, i), [128, 2, TT], BF16, ph)
                    st["hkh", i] = sb("shkh%d" % i, [128, 2, 512], BF16, ph)
                    st["hv", i] = sb("shv%d" % i, [128, 2, 256], BF16, ph)
                    st["nav", i] = sb("snav%d" % i, [128, 2, 384], BF16, ph)
                    st["swv", i] = sb("sswv%d" % i, [128, 2, 384], BF16, ph)
                pss = pst("pss", ph)
                pf = [pst("pf%d" % i, ph) for i in range(2)]
                pt = [pst("pt%d" % i, ph) for i in range(4)]
                prp = pst("prp", ph)

                S.dma("pool", [lambda e, c=c, h_=h_: e.dma_start(out=WIN[:, c, h_ * 1536:(h_ + 1) * 1536], in_=w_in[l, c * 128:(c + 1) * 128, h_ * 1536:(h_ + 1) * 1536])
                               for c in range(8) for h_ in range(2)], writes=[buf("WIN")])
                kvsel = [(0, 0), (0, 1), (1, 1)]
                for c in range(8):
                    for j, (a, b_) in enumerate(kvsel):
                        for hh, kvh in enumerate((a, b_)):
                            S.op("pool", lambda e, c=c, j=j, hh=hh, kvh=kvh: e.tensor_copy(out=WKD[:, c, j * 128 + hh * 64:j * 128 + hh * 64 + 64],
                                                                                      in_=WIN[:, c, 2816 + kvh * 64:2816 + kvh * 64 + 64]),
                                 reads=[buf("WIN")], writes=[buf("WKD")])
                            S.op("pool", lambda e, c=c, j=j, hh=hh, kvh=kvh: e.tensor_copy(out=WVD[:, c, j * 128 + hh * 64:j * 128 + hh * 64 + 64],
                                                                                      in_=WIN[:, c, 2944 + kvh * 64:2944 + kvh * 64 + 64]),
                                 reads=[buf("WIN")], writes=[buf("WVD")])
                S.dma("sp", [lambda e: e.dma_start(out=MQ[0][:], in_=mqf_d), lambda e: e.dma_start(out=MQ[1][:], in_=mqb_d),
                             lambda e: e.dma_start(out=MK[0][:], in_=mkf_d), lambda e: e.dma_start(out=MK[1][:], in_=mkb_d),
                             lambda e: e.dma_start(out=swp32[:], in_=swapm_d), lambda e: e.dma_start(out=lbe[:], in_=lbrow)],
                      writes=[buf("MQ"), buf("MK"), buf("swp32"), buf("lbe")])
                S.op("dve", lambda e: e.tensor_copy(out=SWP[:], in_=swp32[:]), reads=[buf("swp32")], writes=[buf("SWP")])
                S.dma("sp", [lambda e: e.dma_start(out=RC[:], in_=ropec_d)], writes=[buf("RC")])
                S.dma("sp", [lambda e: e.dma_start(out=RSN[:], in_=ropes_d)], writes=[buf("RSN")])
                S.op("act", lambda e: e.activation(out=lbe[:], in_=lbe[:], func=AF.Exp), reads=[buf("lbe")], writes=[buf("lbe")])
                S.op("dve", lambda e: e.tensor_tensor(out=lbs[:], in0=lbe[:, 0:512], in1=lbe[:, 512:1024], op=ALU.add), reads=[buf("lbe")], writes=[buf("lbs")])
                S.op("dve", lambda e: e.tensor_tensor(out=lbs[:], in0=lbs[:], in1=lbe[:, 1024:1536], op=ALU.add), reads=[buf("lbe"), buf("lbs")], writes=[buf("lbs")])
                S.op("dve", lambda e: e.tensor_tensor(out=lbs[:], in0=lbs[:], in1=lbe[:, 1536:2048], op=ALU.add), reads=[buf("lbe"), buf("lbs")], writes=[buf("lbs")])
                S.op("dve", lambda e: e.reciprocal(out=lbs[:], in_=lbs[:]), reads=[buf("lbs")], writes=[buf("lbs")])
                S.op("dve", lambda e: e.memset(LBROW[:], 0.0), writes=[buf("LBROW")])
                for ll in range(1, l + 1):
                    S.op("dve", lambda e, ll=ll: e.tensor_tensor(out=sig[:], in0=lbe[:, ll * 512:(ll + 1) * 512], in1=lbs[:], op=ALU.mult),
                         reads=[buf("lbe"), buf("lbs")], writes=[buf("sig")])
                    S.op("dve", lambda e: e.tensor_tensor(out=LBROW[:], in0=LBROW[:], in1=sig[:], op=ALU.add), reads=[buf("sig"), buf("LBROW")], writes=[buf("LBROW")])
                S.op("dve", lambda e: e.tensor_scalar(out=OMLROW[:], in0=LBROW[:], scalar1=-1.0, scalar2=1.0, op0=ALU.mult, op1=ALU.add),
                     reads=[buf("LBROW")], writes=[buf("OMLROW")])

                def load(i):
                    S.dma("sp", [lambda e: e.dma_start(out=xb[i % 2][:], in_=xs[:, :, i * TT:(i + 1) * TT])], writes=[buf("xb%d" % (i % 2))])

                def nm(i):
                    norm_mod(xb[i % 2], "xb%d" % (i % 2), l, 1, tile_v(i), hx[i % 2], "hx%d" % (i % 2), TT, pss, "pss", sq, rs, tmp8)

                pfc = [0]

                def fm_proj(i, wt, wname, col0):
                    p = pf[pfc[0] % 2]; pn = "pf%d" % (pfc[0] % 2); pfc[0] += 1
                    for k in range(8):
                        S.op("pe", lambda e, p=p, k=k: e.matmul(p[:, 0:TT], lhsT=wt[:, k, col0:col0 + 128], rhs=hx[i % 2][:, k, :], start=(k == 0), stop=(k == 7)),
                             reads=[buf(wname), buf("hx%d" % (i % 2))], writes=[buf(pn)], signal=(k == 7))
                    return p, pn

                ptc = [0]

                def tm_proj(i, u, wt, wname, col0, ncol):
                    p = pt[ptc[0] % 4]; pn = "pt%d" % (ptc[0] % 4); ptc[0] += 1
                    for k in range(8):
                        S.op("pe", lambda e, p=p, k=k: e.matmul(p[:, 0:ncol], lhsT=hx[i % 2][:, k, u * 128:(u + 1) * 128], rhs=wt[:, k, col0:col0 + ncol],
                                                              start=(k == 0), stop=(k == 7)),
                             reads=[buf(wname), buf("hx%d" % (i % 2))], writes=[buf(pn)], signal=(k == 7))
                    return p, pn

                def out_dma(key, i, dst_ap):
                    src = st[key, i % 2]
                    S.dma("sp", [lambda e: e.dma_start(out=dst_ap, in_=src[:])], reads=[buf("st_%s%d" % (key, i % 2))], writes=[buf("dr_" + key)])

                load(0)
                nm(0)

                def tileB(i):
                    ib = i % 2
                    islat = i < NT - 1
                    tsl = slice(i * TT, (i + 1) * TT)
                    if i + 1 < NT:
                        load(i + 1)
                    for pr in range(2):
                        p, pn = fm_proj(i, WIN, "WIN", pr * 128)
                        S.op("act", lambda e, p=p, pr=pr: e.activation(out=qs[pr][:], in_=p[:, 0:TT], func=AF.Silu), reads=[buf(pn)], writes=[buf("qs%d" % pr)])
                    for d_ in range(2):
                        for pr in range(2):
                            p, pn = fm_proj(i, WIN, "WIN", 256 + d_ * 256 + pr * 128)
                            S.op("act", lambda e, p=p: e.activation(out=fsig[:], in_=p[:, 0:TT], func=AF.Sigmoid), reads=[buf(pn)], writes=[buf("fsig")])
                            ci = l * 4 + d_ * 2 + pr
                            S.op("dve", lambda e, d_=d_, pr=pr, ci=ci: e.tensor_scalar(out=kts[d_][pr][:], in0=fsig[:], scalar1=OMLC[:, ci:ci + 1], scalar2=OMBC[:, ci:ci + 1],
                                                                                 op0=ALU.mult, op1=ALU.add),
                                 reads=[buf("fsig"), buf("OMLC"), buf("OMBC")], writes=[buf("kts%d%d" % (d_, pr))])
                    for pr in range(2):
                        p, pn = fm_proj(i, WIN, "WIN", 1024 + pr * 128)
                        S.op("act", lambda e, p=p, pr=pr: e.activation(out=st["hgt", ib][:, pr, :], in_=p[:, 0:TT], func=AF.Silu), reads=[buf(pn)], writes=[buf("st_hgt%d" % ib)])
                    out_dma("hgt", i, hgt[:, :, tsl])
                    for j in range(3):
                        p, pn = fm_proj(i, WIN, "WIN", 1280 + j * 128)
                        S.op("act", lambda e, p=p, j=j: e.activation(out=st["naq", ib][:, j, :], in_=p[:, 0:TT], func=AF.Identity, scale=0.125), reads=[buf(pn)], writes=[buf("st_naq%d" % ib)])
                        p, pn = fm_proj(i, WIN, "WIN", 1664 + j * 128)
                        S.op("dve", lambda e, p=p, j=j: e.tensor_copy(out=st["nak", ib][:, j, :], in_=p[:, 0:TT]), reads=[buf(pn)], writes=[buf("st_nak%d" % ib)])
                    out_dma("naq", i, naq[:, :, tsl]); out_dma("nak", i, nak[:, :, tsl])
                    for (key, wt, wname, base) in (("swq", WIN, "WIN", 2432), ("swk", WKD, "WKD", 0)):
                        for j in range(3):
                            p, pn = fm_proj(i, wt, wname, base + j * 128)
                            dst = st[key, ib]
                            if not islat:
                                S.op("act", lambda e, p=p, j=j, dst=dst: e.activation(out=dst[:, j, :], in_=p[:, 0:TT], func=AF.Identity), reads=[buf(pn)], writes=[buf("st_%s%d" % (key, ib))])
                            else:
                                S.op("act", lambda e, p=p: e.activation(out=xb16[:], in_=p[:, 0:TT], func=AF.Identity), reads=[buf(pn)], writes=[buf("xb16")])
                                S.op("pe", lambda e: e.matmul(prp[:, 0:TT], lhsT=SWP[:], rhs=xb16[:], start=True, stop=True), reads=[buf("SWP"), buf("xb16")], writes=[buf("prp")], signal=True)
                                S.op("act", lambda e, p=p: e.activation(out=x32[:], in_=p[:, 0:TT], func=AF.Identity), reads=[buf(pn)], writes=[buf("x32")])
                                S.op("act", lambda e: e.activation(out=sw32[:], in_=prp[:, 0:TT], func=AF.Identity), reads=[buf("prp")], writes=[buf("sw32")])
                                S.op("pool", lambda e: e.tensor_tensor(out=t2[:], in0=x32[:], in1=RC[:, tsl], op=ALU.mult), reads=[buf("x32"), buf("RC")], writes=[buf("t2")])
                                S.op("pool", lambda e: e.tensor_tensor(out=t1[:], in0=sw32[:], in1=RSN[:, tsl], op=ALU.mult), reads=[buf("sw32"), buf("RSN")], writes=[buf("t1")])
                                S.op("dve", lambda e, j=j, dst=dst: e.tensor_tensor(out=dst[:, j, :], in0=t1[:], in1=t2[:], op=ALU.add), reads=[buf("t1"), buf("t2")], writes=[buf("st_%s%d" % (key, ib))])
                    out_dma("swq", i, swq[:, :, tsl]); out_dma("swk", i, swk[:, :, tsl])
                    for u in range(2):
                        sub = 2 * i + u
                        usl = slice(u * 128, (u + 1) * 128)
                        p, pn = tm_proj(i, u, WIN, "WIN", 256, 512)
                        S.op("act", lambda e, p=p: e.activation(out=sig[:], in_=p[:, 0:512], func=AF.Sigmoid), reads=[buf(pn)], writes=[buf("sig")])
                        S.op("dve", lambda e: e.tensor_tensor(out=sig[:], in0=sig[:], in1=OMLROW[:], op=ALU.mult), reads=[buf("sig"), buf("OMLROW")], writes=[buf("sig")])
                        S.op("dve", lambda e: e.tensor_tensor(out=sig[:], in0=sig[:], in1=LBROW[:], op=ALU.add), reads=[buf("sig"), buf("LBROW")], writes=[buf("sig")])
                        S.op("act", lambda e: e.activation(out=LOGF[:], in_=sig[:], func=AF.Ln), reads=[buf("sig")], writes=[buf("LOGF")])
                        S.op("dve", lambda e: e.tensor_scalar(out=kk[:], in0=sig[:], scalar1=-1.0, scalar2=1.0, op0=ALU.mult, op1=ALU.add), reads=[buf("sig")], writes=[buf("kk")])
                        p2 = pt[ptc[0] % 4]; pn2 = "pt%d" % (ptc[0] % 4); ptc[0] += 1
                        for d_ in range(2):
                            S.op("pe", lambda e, p2=p2, d_=d_: e.matmul(p2[:, d_ * 256:(d_ + 1) * 256], lhsT=MK[d_][:], rhs=LOGF[:, d_ * 256:(d_ + 1) * 256], start=True, stop=True),
                                 reads=[buf("MK"), buf("LOGF")], writes=[buf(pn2)], signal=(d_ == 1))
                        S.op("act", lambda e, p2=p2: e.activation(out=ek[:], in_=p2[:, 0:512], func=AF.Exp), reads=[buf(pn2)], writes=[buf("ek")])
                        S.op("dve", lambda e, u=u: e.tensor_tensor(out=st["hkh", ib][:, u, :], in0=kk[:], in1=ek[:], op=ALU.mult), reads=[buf("kk"), buf("ek")], writes=[buf("st_hkh%d" % ib)])
                        for d_ in range(2):
                            p3 = pt[ptc[0] % 4]; pn3 = "pt%d" % (ptc[0] % 4); ptc[0] += 1
                            for pr in range(2):
                                S.op("pe", lambda e, p3=p3, d_=d_, pr=pr: e.matmul(p3[:, pr * 132:(pr + 1) * 132], lhsT=LOGF[:, d_ * 256 + pr * 128:d_ * 256 + (pr + 1) * 128], rhs=MQ[d_][:],
                                                                               start=True, stop=True),
                                     reads=[buf("MQ"), buf("LOGF")], writes=[buf(pn3)], signal=(pr == 1))
                            for pr in range(2):
                                ea = e1[0]; eb = e1[1]
                                S.op("act", lambda e, p3=p3, pr=pr: e.activation(out=ea[:], in_=p3[:, pr * 132:pr * 132 + 128], func=AF.Exp), reads=[buf(pn3)], writes=[buf("e1_0")])
                                S.op("act", lambda e, p3=p3, pr=pr: e.activation(out=eb[:], in_=p3[:, pr * 132:pr * 132 + 128], func=AF.Exp, scale=-1.0), reads=[buf(pn3)], writes=[buf("e1_1")])
                                S.op("act", lambda e, p3=p3, pr=pr, d_=d_, sub=sub: e.activation(out=EXPL[:, d_, pr, 2 * sub:2 * sub + 2], in_=p3[:, pr * 132 + 128:pr * 132 + 130], func=AF.Exp),
                                     reads=[buf(pn3)], writes=[buf("EXPL")])
                                S.op("act", lambda e, p3=p3, pr=pr, d_=d_, sub=sub: e.activation(out=EMID[:, d_, pr, 2 * sub:2 * sub + 2], in_=p3[:, pr * 132 + 130:pr * 132 + 132], func=AF.Exp),
                                     reads=[buf(pn3)], writes=[buf("EMID")])
                                S.op("dve", lambda e, pr=pr, d_=d_, usl=usl: e.tensor_tensor(out=st["hqt%d" % d_, ib][:, pr, usl], in0=qs[pr][:, usl], in1=ea[:], op=ALU.mult),
                                     reads=[buf("qs%d" % pr), buf("e1_0")], writes=[buf("st_hqt%d%d" % (d_, ib))])
                                S.op("dve", lambda e, pr=pr, d_=d_, usl=usl: e.tensor_tensor(out=st["hkt%d" % d_, ib][:, pr, usl], in0=kts[d_][pr][:, usl], in1=eb[:], op=ALU.mult),
                                     reads=[buf("kts%d%d" % (d_, pr)), buf("e1_1")], writes=[buf("st_hkt%d%d" % (d_, ib))])
                        p, pn = tm_proj(i, u, WIN, "WIN", 768, 256)
                        S.op("act", lambda e, p=p, u=u: e.activation(out=st["hv", ib][:, u, :], in_=p[:, 0:256], func=AF.Identity), reads=[buf(pn)], writes=[buf("st_hv%d" % ib)])
                        p, pn = tm_proj(i, u, WIN, "WIN", 2048, 384)
                        S.op("dve", lambda e, p=p, u=u: e.tensor_copy(out=st["nav", ib][:, u, :], in_=p[:, 0:384]), reads=[buf(pn)], writes=[buf("st_nav%d" % ib)])
                        p, pn = tm_proj(i, u, WVD, "WVD", 0, 384)
                        S.op("act", lambda e, p=p, u=u: e.activation(out=st["swv", ib][:, u, :], in_=p[:, 0:384], func=AF.Identity), reads=[buf(pn)], writes=[buf("st_swv%d" % ib)])
                    for d_ in range(2):
                        out_dma("hqt%d" % d_, i, hqt[d_][:, :, tsl]); out_dma("hkt%d" % d_, i, hkt[d_][:, :, tsl])
                    out_dma("hkh", i, hkh[:, 2 * i:2 * i + 2, :]); out_dma("hv", i, hv[:, 2 * i:2 * i + 2, :])
                    out_dma("nav", i, nav[:, 2 * i:2 * i + 2, :]); out_dma("swv", i, swv[:, 2 * i:2 * i + 2, :])
                    if i + 1 < NT:
                        nm(i + 1)

                for i in range(NT):
                    tileB(i)
                S.barrier()
                S.flush()


        def attn_phase(l, last, kind):
            with contextlib.ExitStack() as ph:
                Q = sb("Q", [128, 2, 3, T], BF16, ph); Kt = sb("Kt", [128, 3, T], BF16, ph); V = sb("V", [128, NS, 384], BF16, ph)
                if kind == "na":
                    NAB = sb("NAB", [128, NKIND, 768], BF16, ph)
                else:
                    SWM = sb("SWM", [128, 2, 256], BF16, ph); swm32 = sb("swm32", [128, 2, 256], F32, ph)
                PT = [sb("PT%d" % i, [128, 256], BF16, ph) for i in range(3)]
                rec = [sb("rec%d" % i, [128, 256], F32, ph) for i in range(2)]
                stg = [sb("stg%d" % i, [128, 3, 128], BF16, ph) for i in range(2)]
                pS = [pst("pS%d" % i, ph) for i in range(3)]
                pO = [pst("pO%d" % i, ph) for i in range(2)]
                pD = [pst("pD%d" % i, ph) for i in range(2)]
                qsrc, ksrc, vsrc = (naq, nak, nav) if kind == "na" else (swq, swk, swv)
                S.op("dve", lambda e: e.memset(Q[:], 0.0), writes=[buf("Q")])
                S.dma("sp", [lambda e, j=j, h=h: e.dma_start(out=Q[h * 64:(h + 1) * 64, h, j, :], in_=qsrc[h * 64:(h + 1) * 64, j, :]) for j in range(3) for h in range(2)],
                      writes=[buf("Q")])
                S.dma("sp", [lambda e, j=j: e.dma_start(out=Kt[:, j, :], in_=ksrc[:, j, :]) for j in range(3)], writes=[buf("Kt")])
                S.dma("sp", [lambda e, a=a: e.dma_start(out=V[:, a * 17:(a + 1) * 17, :], in_=vsrc[:, a * 17:(a + 1) * 17, :]) for a in range(2)], writes=[buf("V")])
                if kind == "na":
                    S.dma("pool", [lambda e, ki=ki: e.dma_start(out=NAB[:, ki, :], in_=nabias[l, ki]) for ki in range(NKIND)], writes=[buf("NAB")])
                else:
                    S.dma("sp", [lambda e: e.dma_start(out=swm32[:], in_=swm_d)], writes=[buf("swm32")])
                    S.op("dve", lambda e: e.tensor_copy(out=SWM[:], in_=swm32[:]), reads=[buf("swm32")], writes=[buf("SWM")])
                base = 2 if kind == "na" else 5
                scale = 1.0 if kind == "na" else 0.125
                cS = [0]; cO = [0]

                def qtile(qt):
                    sg = stg[qt % 2]; sgn = "stg%d" % (qt % 2)
                    qsl = slice(qt * 128, (qt + 1) * 128)
                    if qt >= 32:
                        keys = [(32, None), (33, None)]
                    elif kind == "na":
                        keys = [(n, ("na", ki)) for (n, ki) in NA_PER_Q[qt]] + [(32, None), (33, None)]
                    else:
                        keys = []
                        if qt - 1 >= 0:
                            keys.append((qt - 1, ("sw", 0)))
                        keys.append((qt, None))
                        if qt + 1 <= 31:
                            keys.append((qt + 1, ("sw", 1)))
                        keys += [(32, None), (33, None)]
                    for j in range(3):
                        po = pO[cO[0] % 2]; pon = "pO%d" % (cO[0] % 2)
                        pd = pD[cO[0] % 2]; pdn = "pD%d" % (cO[0] % 2)
                        rc_ = rec[cO[0] % 2]; rcn = "rec%d" % (cO[0] % 2)
                        cO[0] += 1
                        nk = len(keys)
                        for ix, (kt, bias) in enumerate(keys):
                            ps = pS[cS[0] % 3]; psn = "pS%d" % (cS[0] % 3)
                            ptb = PT[cS[0] % 3]; ptn = "PT%d" % (cS[0] % 3)
                            cS[0] += 1
                            ksl = slice(kt * 128, (kt + 1) * 128)
                            for h in range(2):
                                S.op("pe", lambda e, ps=ps, h=h, ksl=ksl, j=j, bias=bias: e.matmul(
                                    ps[:, h * 128:(h + 1) * 128], lhsT=Kt[:, j, ksl], rhs=Q[:, h, j, qsl],
                                    start=True, stop=(bias is None)), reads=[buf("Kt"), buf("Q")], writes=[buf(psn)], signal=(bias is None and h == 1))
                                if bias is not None and bias[0] == "na":
                                    S.op("pe", lambda e, ps=ps, h=h, j=j, ki=bias[1]: e.matmul(
                                        ps[:, h * 128:(h + 1) * 128], lhsT=identb[:], rhs=NAB[:, ki, (2 * j + h) * 128:(2 * j + h + 1) * 128],
                                        start=False, stop=True), reads=[buf("identb"), buf("NAB")], writes=[buf(psn)], signal=(h == 1))
                                elif bias is not None:
                                    S.op("pe", lambda e, ps=ps, h=h, w=bias[1]: e.matmul(
                                        ps[:, h * 128:(h + 1) * 128], lhsT=identb[:], rhs=SWM[:, w, h * 128:(h + 1) * 128],
                                        start=False, stop=True), reads=[buf("identb"), buf("SWM")], writes=[buf(psn)], signal=(h == 1))
                            S.op("act", lambda e, ps=ps, ptb=ptb: e.activation(out=ptb[:], in_=ps[:, 0:256], func=AF.Exp, scale=scale),
                                 reads=[buf(psn)], writes=[buf(ptn)])
                            S.op("pe", lambda e, po=po, ptb=ptb, kt=kt, j=j, ix=ix, nk=nk: e.matmul(
                                po[:, 0:256], lhsT=V[:, kt, j * 128:(j + 1) * 128], rhs=ptb[:], start=(ix == 0), stop=(ix == nk - 1)),
                                reads=[buf("V"), buf(ptn)], writes=[buf(pon)], signal=(ix == nk - 1))
                            S.op("pe", lambda e, pd=pd, ptb=ptb, ix=ix, nk=nk: e.matmul(
                                pd[:, 0:256], lhsT=onesb[:], rhs=ptb[:], start=(ix == 0), stop=(ix == nk - 1)),
                                reads=[buf("onesb"), buf(ptn)], writes=[buf(pdn)], signal=(ix == nk - 1))
                        if kind == "sw":
                            for h in range(2):
                                si = l * 6 + 2 * j + h
                                S.op("dve", lambda e, pd=pd, rc_=rc_, h=h, si=si: e.tensor_scalar(
                                    out=rc_[:, h * 128:(h + 1) * 128], in0=pd[:, h * 128:(h + 1) * 128], scalar1=SINKE[:, si:si + 1], scalar2=None, op0=ALU.add),
                                    reads=[buf(pdn), buf("SINKE")], writes=[buf(rcn)])
                            S.op("dve", lambda e, rc_=rc_: e.reciprocal(out=rc_[:], in_=rc_[:]), reads=[buf(rcn)], writes=[buf(rcn)])
                        else:
                            S.op("dve", lambda e, pd=pd, rc_=rc_: e.reciprocal(out=rc_[:], in_=pd[:, 0:256]), reads=[buf(pdn)], writes=[buf(rcn)])
                        for h in range(2):
                            S.op("dve", lambda e, po=po, rc_=rc_, h=h, j=j, sg=sg: e.tensor_tensor(
                                out=sg[h * 64:(h + 1) * 64, j, :], in0=po[h * 64:(h + 1) * 64, h * 128:(h + 1) * 128],
                                in1=rc_[h * 64:(h + 1) * 64, h * 128:(h + 1) * 128], op=ALU.mult),
                                reads=[buf(pon), buf(rcn)], writes=[buf(sgn)])
                    S.dma("sp", [lambda e, sg=sg: e.dma_start(out=omix[:, base:base + 3, qsl], in_=sg[:])], reads=[buf(sgn)], writes=[buf("dr_omix")])

                for qt in range(32 if last else 34):
                    qtile(qt)
                S.barrier()
                S.flush()

        def hg_phase(l, last):
            with contextlib.ExitStack() as ph:
                QT = [sb("QT%d" % d_, [128, 2, T], BF16, ph) for d_ in range(2)]
                KT = [sb("KT%d" % d_, [128, 2, 2, T], BF16, ph) for d_ in range(2)]
                IND = sb("IND", [128, 2], F32, ph); vz = [sb("vz%d" % i, [128, 256], BF16, ph) for i in range(2)]
                hgl = [sb("hgl%d" % i, [128, 128], BF16, ph) for i in range(2)]
                KH = sb("KH", [128, NS, 512], BF16, ph); HV = sb("HV", [128, NS, 256], BF16, ph)
                OF = sb("OF", [128, 2, T], F32, ph)
                MASK = [sb("MASK%d" % d_, [128, 256], F32, ph) for d_ in range(2)]
                S32 = [sb("S32_%d" % p_, [128, 128], F32, ph) for p_ in range(2)]
                SBF = [[sb("SBF%d%d" % (p_, k_), [128, 128], BF16, ph) for k_ in range(2)] for p_ in range(2)]
                Acl = [sb("Acl%d" % i, [128, 256], F32, ph) for i in range(2)]
                Asb = [sb("Asb%d" % i, [128, 256], BF16, ph) for i in range(2)]
                o32 = [sb("o32_%d" % i, [128, 128], F32, ph) for i in range(2)]
                sqo = [sb("sqo%d" % i, [128, 128], F32, ph) for i in range(2)]
                rr = [sb("rr%d" % i, [128, 128], F32, ph) for i in range(2)]
                stg = [sb("hstg%d" % i, [128, 128], BF16, ph) for i in range(2)]
                pU = [pst("pU%d" % i, ph) for i in range(1)] * 2
                pA = [pst("pA%d" % i, ph) for i in range(2)]
                pO = [[pst("pO%d%d" % (i, h), ph) for h in range(2)] for i in range(2)]
                pN = [pst("pN%d" % i, ph) for i in range(1)] * 2
                for d_ in range(2):
                    S.dma("sp", [lambda e, d_=d_, j=j: e.dma_start(out=QT[d_][:, j, :], in_=hqt[d_][:, j, :]) for j in range(2)], writes=[buf("QT%d" % d_)])
                    S.op("dve", lambda e, d_=d_: e.memset(KT[d_][:], 0.0), writes=[buf("KT%d" % d_)])
                    S.dma("sp", [lambda e, d_=d_, j=j, h=h: e.dma_start(out=KT[d_][h * 64:(h + 1) * 64, h, j, :], in_=hkt[d_][h * 64:(h + 1) * 64, j, :]) for j in range(2) for h in range(2)],
                          writes=[buf("KT%d" % d_)])
                S.dma("sp", [lambda e, a=a: e.dma_start(out=KH[:, a * 17:(a + 1) * 17, :], in_=hkh[:, a * 17:(a + 1) * 17, :]) for a in range(2)], writes=[buf("KH")])
                S.dma("sp", [lambda e: e.dma_start(out=HV[:], in_=hv)], writes=[buf("HV")])
                S.op("dve", lambda e: e.memset(IND[:], 0.0), writes=[buf("IND")])
                S.op("dve", lambda e: e.memset(IND[0:64, 0:1], 1.0), writes=[buf("IND")])
                S.op("dve", lambda e: e.memset(IND[64:128, 1:2], 1.0), writes=[buf("IND")])
                S.dma("sp", [lambda e: e.dma_start(out=MASK[0][:], in_=maskf_d), lambda e: e.dma_start(out=MASK[1][:], in_=maskb_d)], writes=[buf("MASK")])
                cnt_ = [0]
                for d_ in range(2):
                    order = [32, 33] + list(range(32)) if d_ == 0 else [33, 32] + list(range(31, -1, -1))
                    chunks = []
                    for sub in order:
                        for c in ((0, 1) if d_ == 0 else (1, 0)):
                            chunks.append(2 * sub + c)
                    for pr in range(2):
                        S.op("dve", lambda e, pr=pr: e.memset(S32[pr][:], 0.0), writes=[buf("S32_%d" % pr)])
                        S.op("dve", lambda e, pr=pr: e.memset(SBF[pr][0][:], 0.0), writes=[buf("SBF%d0" % pr)])
                        S.op("dve", lambda e, pr=pr: e.memset(SBF[pr][1][:], 0.0), writes=[buf("SBF%d1" % pr)])
                    ncount = [0, 0]
                    for si, sub in enumerate(order):
                        for pr in range(2):
                            hg_step(d_, pr, sub, si, chunks, ncount, l, last,
                                    dict(S=S, buf=buf, pU=pU, pA=pA, pO=pO, pN=pN, KH=KH, HV=HV, KT=KT, QT=QT, Acl=Acl, Asb=Asb, MASK=MASK, SBF=SBF, S32=S32,
                                         EXPL=EXPL, EMID=EMID, OF=OF, o32=o32, sqo=sqo, rr=rr, stg=stg, IND=IND, vz=vz, hgl=hgl, hgt=hgt, HGN=HGN, bones=bones, epsb=epsb, omix=omix, cnt_=cnt_))
                S.barrier()
                S.flush()

        def phase_C(l, last):
            if os.environ.get("CUTC"):
                S.cut = S.nrec + int(os.environ["CUTC"])
            attn_phase(l, last, "na")
            attn_phase(l, last, "sw")
            hg_phase(l, last)
        for l in range(depth):
            last = (l == depth - 1)
            if dbg == "P0":
                break
            phase_A(l)
            if dbg == "A":
                break
            if dbg != "AD":
                phase_B(l)
                if dbg == "B":
                    break
                phase_C(l, last)
            phase_D(l, last)
        S.barrier()
        S.finish()
    return nc


def hg_step(d_, pr, sub, si, chunks, ncount, l, last, X):
    S = X["S"]; buf = X["buf"]
    KH = X["KH"]; HV = X["HV"]; KT = X["KT"]; QT = X["QT"]; MASK = X["MASK"]; SBF = X["SBF"]; S32 = X["S32"]
    EXPL = X["EXPL"]; EMID = X["EMID"]; OF = X["OF"]; HGN = X["HGN"]; bones = X["bones"]; epsb = X["epsb"]; omix = X["omix"]
    IND = X["IND"]; hgt = X["hgt"]
    c_ = X["cnt_"][0] % 2
    X["cnt_"][0] += 1
    pu = X["pU"][c_]; pun = "pU0"
    pa = X["pA"][c_]; pan = "pA%d" % c_
    po = X["pO"][c_]; pon = "pO%d" % c_
    pn = X["pN"][c_]; pnn = "pN0"
    Acl = X["Acl"][c_]; Asb = X["Asb"][c_]; vz = X["vz"][c_]; hgl = X["hgl"][c_]
    o32 = X["o32"][c_]; sqo = X["sqo"][c_]; rr = X["rr"][c_]; stg = X["stg"][c_]
    tsl = slice(sub * 128, (sub + 1) * 128)
    psl = slice(pr * 128, (pr + 1) * 128)
    for c in range(2):
        S.op("dve", lambda e, c=c: e.tensor_scalar(out=vz[:, c * 128:(c + 1) * 128], in0=HV[:, sub, psl], scalar1=IND[:, c:c + 1], scalar2=None, op0=ALU.mult),
             reads=[buf("HV"), buf("IND")], writes=[buf("vz%d" % c_)])
    for c in range(2):
        S.op("pe", lambda e, c=c: e.matmul(pu[:, c * 128:(c + 1) * 128], lhsT=KH[:, sub, d_ * 256 + pr * 128:d_ * 256 + (pr + 1) * 128],
                                           rhs=vz[:, c * 128:(c + 1) * 128], start=True, stop=True),
             reads=[buf("KH"), buf("vz%d" % c_)], writes=[buf(pun)], signal=(c == 1))
    for h in range(2):
        S.op("pe", lambda e, h=h: e.matmul(pa[:, h * 128:(h + 1) * 128], lhsT=KT[d_][:, h, pr, tsl], rhs=QT[d_][:, pr, tsl], start=True, stop=True),
             reads=[buf("KT%d" % d_), buf("QT%d" % d_)], writes=[buf(pan)], signal=(h == 1))
    S.op("dve", lambda e: e.tensor_scalar(out=Acl[:], in0=pa[:, 0:256], scalar1=-1e30, scalar2=1e30, op0=ALU.max, op1=ALU.min),
         reads=[buf(pan)], writes=[buf("Acl%d" % c_)])
    S.op("dve", lambda e: e.tensor_tensor(out=Asb[:], in0=Acl[:], in1=MASK[d_][:], op=ALU.mult),
         reads=[buf("Acl%d" % c_), buf("MASK")], writes=[buf("Asb%d" % c_)])
    for h in range(2):
        S.op("pe", lambda e, h=h: e.matmul(po[h][:, 0:128], lhsT=HV[:, sub, psl], rhs=Asb[:, h * 128:(h + 1) * 128], start=True, stop=False),
             reads=[buf("HV"), buf("Asb%d" % c_)], writes=[buf(pon)])
    for kk in range(2):
        ch = chunks[2 * si + kk]
        c = ch % 2
        for h in range(2):
            S.op("pe", lambda e, h=h, c=c, kk=kk: e.matmul(
                po[h][:, c * 64:(c + 1) * 64], lhsT=SBF[pr][kk][:],
                rhs=QT[d_][:, pr, sub * 128 + c * 64:sub * 128 + (c + 1) * 64], start=False, stop=True),
                reads=[buf("SBF%d%d" % (pr, kk)), buf("QT%d" % d_)], writes=[buf(pon)], signal=(h == 1 and kk == 1))
        S.op("dve", lambda e, c=c, ch=ch: e.scalar_tensor_tensor(out=S32[pr][:], in0=S32[pr][:], scalar=EXPL[:, d_, pr, ch:ch + 1],
                                                               in1=pu[:, c * 128:(c + 1) * 128], op0=ALU.mult, op1=ALU.add),
             reads=[buf("S32_%d" % pr), buf(pun), buf("EXPL")], writes=[buf("S32_%d" % pr)])
        nxt_i = 2 * si + kk + 1
        if nxt_i < len(chunks):
            chn = chunks[nxt_i]
            slot = (kk + 1) % 2
            for h in range(2):
                hs = slice(h * 64, (h + 1) * 64)
                S.op("act", lambda e, chn=chn, slot=slot, hs=hs: e.activation(out=SBF[pr][slot][hs, hs], in_=S32[pr][hs, hs], func=AF.Identity,
                                                                           scale=EMID[hs, d_, pr, chn:chn + 1]),
                     reads=[buf("S32_%d" % pr), buf("EMID")], writes=[buf("SBF%d%d" % (pr, slot))])
    if d_ == 0:
        for h in range(2):
            hs = slice(h * 64, (h + 1) * 64)
            S.op("act", lambda e, h=h, hs=hs: e.activation(out=OF[hs, pr, tsl], in_=po[h][hs, 0:128], func=AF.Identity), reads=[buf(pon)], writes=[buf("OF")])
        return
    if last and sub >= 32:
        return
    S.dma("sp", [lambda e: e.dma_start(out=hgl[:], in_=hgt[:, pr, tsl])], writes=[buf("hgl%d" % c_)])
    for h in range(2):
        hs = slice(h * 64, (h + 1) * 64)
        S.op("dve", lambda e, h=h, hs=hs: e.tensor_tensor(out=o32[hs, :], in0=po[h][hs, 0:128], in1=OF[hs, pr, tsl], op=ALU.add),
             reads=[buf(pon), buf("OF")], writes=[buf("o32_%d" % c_)])
    S.op("act", lambda e: e.activation(out=sqo[:], in_=o32[:], func=AF.Square), reads=[buf("o32_%d" % c_)], writes=[buf("sqo%d" % c_)])
    S.op("pe", lambda e: e.matmul(pn[:, 0:128], lhsT=bones[:], rhs=sqo[:], start=True, stop=True), reads=[buf("bones"), buf("sqo%d" % c_)], writes=[buf(pnn)], signal=True)
    S.op("act", lambda e: e.activation(out=rr[:], in_=pn[:, 0:128], func=AF.Sqrt, scale=1.0 / 64, bias=epsb[:]), reads=[buf(pnn), buf("epsb")], writes=[buf("rr%d" % c_)])
    S.op("dve", lambda e: e.reciprocal(out=rr[:], in_=rr[:]), reads=[buf("rr%d" % c_)], writes=[buf("rr%d" % c_)])
    S.op("dve", lambda e: e.tensor_tensor(out=o32[:], in0=o32[:], in1=rr[:], op=ALU.mult), reads=[buf("o32_%d" % c_), buf("rr%d" % c_)], writes=[buf("o32_%d" % c_)])
    hi = l * 2 + pr
    S.op("dve", lambda e: e.scalar_tensor_tensor(out=stg[:], in0=o32[:], scalar=HGN[:, hi:hi + 1], in1=hgl[:], op0=ALU.mult, op1=ALU.mult),
         reads=[buf("o32_%d" % c_), buf("HGN"), buf("hgl%d" % c_)], writes=[buf("hstg%d" % c_)])
    S.dma("sp", [lambda e: e.dma_start(out=omix[:, pr, tsl], in_=stg[:])], reads=[buf("hstg%d" % c_)], writes=[buf("dr_omix")])


def host_inputs(inp, b, depth=DEPTH):
    f32 = np.float32
    x = np.asarray(inp["x"][b], f32); ctx = np.asarray(inp["ctx"][b], f32)
    seq = np.concatenate([x, ctx], 0)
    xin = np.ascontiguousarray(seq.T.reshape(8, 128, T).transpose(1, 0, 2))
    c = np.asarray(inp["c"][b], f32); c_ctx = np.asarray(inp["c_ctx"], f32)
    cc = np.stack([c.reshape(8, 128).T, c_ctx.reshape(8, 128).T], -1)
    ada_b = np.asarray(inp["ada_b"], f32)
    adab = ada_b.reshape(DEPTH, 9, 8, 128).transpose(3, 0, 1, 2)
    adab2 = np.repeat(adab[..., None], 2, -1).reshape(128, DEPTH, 144)
    ng = np.asarray(inp["norm_g"], f32).reshape(DEPTH, 3, 8, 128).transpose(3, 0, 1, 2)
    normg2 = np.repeat(ng[..., None], 2, -1).reshape(128, DEPTH, 3, 16)
    finalg = np.asarray(inp["final_g"], f32).reshape(8, 128).T
    lb = np.asarray(inp["hg_lb_logits"], f32)
    lbrow = np.broadcast_to(lb.reshape(1, DEPTH * 512), (128, DEPTH * 512))
    lbcol = lb.reshape(DEPTH, 2, 2, 128).transpose(3, 0, 1, 2).reshape(128, DEPTH * 4)
    hgng = np.asarray(inp["hg_norm_g"], f32).reshape(DEPTH, 2, 128).transpose(2, 0, 1).reshape(128, DEPTH * 2)
    rpb = np.asarray(inp["na_rpb"], f32)
    nabias = np.stack([_na_bias_host(rpb[l]) for l in range(depth)], 0).reshape(depth, NKIND, 128, 768)
    sink = np.asarray(inp["sw_sink"], f32).reshape(1, DEPTH * 6)
    sinkrep = np.broadcast_to(sink, (128, DEPTH * 6))
    mqf, mqb, mkf, mkb, maskf, maskb = _hg_consts()
    ct, st, P = _rope_tables()
    r = np.arange(128)
    prev = np.where(r[:, None] >= r[None, :], 0.0, NEG).astype(f32)
    nxt = np.where(r[:, None] <= r[None, :], 0.0, NEG).astype(f32)
    swmask = np.stack([np.concatenate([prev, prev], 1), np.concatenate([nxt, nxt], 1)], 1)
    bones = (r[:, None] // 64 == r[None, :] // 64).astype(f32)
    d = dict(xin=xin, cc=cc, adab2=adab2, normg2=normg2, finalg=finalg, lbrow=lbrow, lbcol=lbcol, hgng=hgng,
             nabias=nabias, sinkrep=sinkrep, mqf=mqf, mqb=mqb, mkf=mkf, mkb=mkb,
             maskf=maskf.reshape(128, 256), maskb=maskb.reshape(128, 256), ropec=ct, ropes=st, swapm=P,
             swmask=swmask, ident=np.eye(128, dtype=f32), bones=bones)
    for k in ("ada_w", "ffn_w1", "ffn_w3", "ffn_w2", "w_in", "w_out"):
        d[k] = np.asarray(inp[k], f32)[:depth]
    return {k: np.ascontiguousarray(v) for k, v in d.items()}


def kernel(**inputs):
    nc = build()
    shared = None
    in_maps = []
    for core in range(8):
        b = core % 4
        if core < 4:
            m = host_inputs(inputs, b)
            if shared is None:
                shared = m
            else:
                for k in ("ada_w", "ffn_w1", "ffn_w3", "ffn_w2", "w_in", "w_out", "nabias"):
                    m[k] = shared[k]
            in_maps.append(m)
        else:
            in_maps.append(in_maps[b])
    res = run_bass_kernel_spmd(nc, in_maps, core_ids=list(range(8)))
    outs = []
    for b in range(4):
        o = res.results[b]["out"]
        outs.append(o.transpose(1, 0, 2).reshape(D, T_LAT).T)
    return np.ascontiguousarray(np.stack(outs, 0).astype(np.float32))
```
